# Optimizing a Trainium2 kernel written in Bass

```python
import math
import jax, jax.numpy as jnp
from jax import lax
import numpy as np

D_MODEL = 1024
BATCH = 32
SEQ = 2048
DEPTH = 2

N_MIXERS = 2
CONV_WIDTH = 3
HEAD_DIM = 64
N_Q_HEADS = D_MODEL // HEAD_DIM
N_KV_HEADS = 4
GROUP = N_Q_HEADS // N_KV_HEADS
WINDOW = 128
BLOCK = 128
D_FF = ((8 * D_MODEL // 3 + 255) // 256) * 256
QKV_WIDTH = (N_Q_HEADS + 2 * N_KV_HEADS) * HEAD_DIM
N_CONV_LAYERS = (DEPTH + 1) // 2
N_ATTN_LAYERS = DEPTH // 2
EPS = 1e-6

kernel_name = "hybrid_shortconv_swa_sink_alibi_swiglu"


def rmsnorm(x, gain):
    xf = x.astype(jnp.float32)
    r = lax.rsqrt(jnp.mean(xf * xf, axis=-1, keepdims=True) + EPS)
    return (xf * r).astype(x.dtype) * gain


def alibi_slopes():
    h = jnp.arange(1, N_Q_HEADS + 1, dtype=jnp.float32)
    return jnp.exp2(-8.0 * h / N_Q_HEADS)


def short_conv_mixer(h, w_in, conv_w, w_out):
    d = h.shape[-1]
    bcx = h @ w_in
    b_gate, c_gate, xv = jnp.split(bcx, 3, axis=-1)
    u = b_gate * xv
    y = lax.conv_general_dilated(
        u, conv_w[:, None, :].astype(u.dtype),
        window_strides=(1,), padding=[(CONV_WIDTH - 1, 0)],
        dimension_numbers=('NWC', 'WIO', 'NWC'), feature_group_count=d)
    return (c_gate * y) @ w_out


def swa_sink_attention(h, w_qkv, q_gain, k_gain, sinks, w_o):
    bsz, s, _ = h.shape
    nb = s // BLOCK
    qkv = h @ w_qkv
    q_end = N_Q_HEADS * HEAD_DIM
    k_end = q_end + N_KV_HEADS * HEAD_DIM
    q = qkv[..., :q_end].reshape(bsz, s, N_KV_HEADS, GROUP, HEAD_DIM)
    k = qkv[..., q_end:k_end].reshape(bsz, s, N_KV_HEADS, HEAD_DIM)
    v = qkv[..., k_end:].reshape(bsz, s, N_KV_HEADS, HEAD_DIM)
    q = rmsnorm(q, q_gain)
    k = rmsnorm(k, k_gain)

    qb = q.reshape(bsz, nb, BLOCK, N_KV_HEADS, GROUP, HEAD_DIM)

    def band(t):
        tp = jnp.pad(t, ((0, 0), (BLOCK, 0), (0, 0), (0, 0)))
        tb = tp.reshape(bsz, nb + 1, BLOCK, N_KV_HEADS, HEAD_DIM)
        return jnp.concatenate([tb[:, :-1], tb[:, 1:]], axis=2)

    kw = band(k)
    vw = band(v)

    scale = 1.0 / math.sqrt(HEAD_DIM)
    scores = jnp.einsum('bnqkgd,bnskd->bnkgqs', qb, kw).astype(jnp.float32) * scale

    qi = jnp.arange(BLOCK)[:, None]
    kj = jnp.arange(2 * BLOCK)[None, :]
    dist = qi + BLOCK - kj
    key_pos = (jnp.arange(nb) * BLOCK - BLOCK)[:, None, None] + kj[None]
    mask = (dist >= 0)[None] & (dist < WINDOW)[None] & (key_pos >= 0)

    slopes = alibi_slopes().reshape(N_KV_HEADS, GROUP)
    alibi = -slopes[:, :, None, None] * dist.astype(jnp.float32)[None, None]
    scores = scores + alibi[None, None]
    scores = jnp.where(mask[None, :, None, None], scores, -jnp.inf)

    sink = sinks.astype(jnp.float32).reshape(N_KV_HEADS, GROUP)
    sink_col = jnp.broadcast_to(sink[None, None, :, :, None, None],
                                scores.shape[:-1] + (1,))
    logits = jnp.concatenate([scores, sink_col], axis=-1)
    p = jax.nn.softmax(logits, axis=-1)[..., :-1]

    out = jnp.einsum('bnkgqs,bnskd->bnqkgd', p.astype(vw.dtype), vw)
    out = out.reshape(bsz, s, N_Q_HEADS * HEAD_DIM)
    return out @ w_o


def swiglu_ffn(h, w_gate_up, w_down):
    gu = h @ w_gate_up
    g, u = jnp.split(gu, 2, axis=-1)
    return (jax.nn.silu(g) * u) @ w_down


def setup_inputs(seed: int = 0) -> dict:
    key = jax.random.key(seed)
    ks = jax.random.split(key, 14)
    f32 = jnp.float32
    d = D_MODEL

    def w(k, shape, fan_in):
        return jax.random.normal(k, shape, f32) * (fan_in ** -0.5)

    return {
        "x": jax.random.normal(ks[0], (BATCH, SEQ, d), f32),
        "conv_w_in": w(ks[1], (N_CONV_LAYERS, d, 3 * d), d),
        "conv_w": w(ks[2], (N_CONV_LAYERS, CONV_WIDTH, d), CONV_WIDTH),
        "conv_w_out": w(ks[3], (N_CONV_LAYERS, d, d), d),
        "attn_w_qkv": w(ks[4], (N_ATTN_LAYERS, d, QKV_WIDTH), d),
        "attn_q_gain": 1.0 + 0.05 * jax.random.normal(ks[5], (N_ATTN_LAYERS, HEAD_DIM), f32),
        "attn_k_gain": 1.0 + 0.05 * jax.random.normal(ks[6], (N_ATTN_LAYERS, HEAD_DIM), f32),
        "attn_sinks": 0.5 * jax.random.normal(ks[7], (N_ATTN_LAYERS, N_Q_HEADS), f32),
        "attn_w_o": w(ks[8], (N_ATTN_LAYERS, N_Q_HEADS * HEAD_DIM, d), N_Q_HEADS * HEAD_DIM),
        "norm_mixer": 1.0 + 0.05 * jax.random.normal(ks[9], (DEPTH, d), f32),
        "norm_ffn": 1.0 + 0.05 * jax.random.normal(ks[10], (DEPTH, d), f32),
        "ffn_w_gate_up": w(ks[11], (DEPTH, d, 2 * D_FF), d),
        "ffn_w_down": w(ks[12], (DEPTH, D_FF, d), D_FF),
    }


def reference(x, conv_w_in, conv_w, conv_w_out, attn_w_qkv, attn_q_gain, attn_k_gain,
              attn_sinks, attn_w_o, norm_mixer, norm_ffn, ffn_w_gate_up, ffn_w_down):
    for i in range(DEPTH):
        h = rmsnorm(x, norm_mixer[i])
        j = i // N_MIXERS
        if i % N_MIXERS == 0:
            mix = short_conv_mixer(h, conv_w_in[j], conv_w[j], conv_w_out[j])
        else:
            mix = swa_sink_attention(h, attn_w_qkv[j], attn_q_gain[j], attn_k_gain[j],
                                     attn_sinks[j], attn_w_o[j])
        x = x + mix
        h = rmsnorm(x, norm_ffn[i])
        x = x + swiglu_ffn(h, ffn_w_gate_up[i], ffn_w_down[i])
    return x
```

```python
import math
from contextlib import ExitStack

import numpy as np
import concourse.bass as bass
import concourse.mybir as mybir
from concourse.bass_utils import run_bass_kernel_spmd

F32 = mybir.dt.float32
BF16 = mybir.dt.bfloat16
AF = mybir.ActivationFunctionType
ALU = mybir.AluOpType

D_MODEL = 1024
D_FF = 2816
NKF = D_FF // 128
HEAD_DIM = 64
N_Q = 16
N_KV = 4
EPS = 1e-6
TILE = 512
NB_RING = 5
SLAB_ELEMS = 4096


class _Op:
    __slots__ = ("idx", "eng", "fn", "deps", "dma_key", "pos", "sigval", "signal")

    def __init__(self, idx, eng, fn, deps, dma_key):
        self.idx, self.eng, self.fn, self.deps, self.dma_key = idx, eng, fn, deps, dma_key
        self.pos = 0
        self.sigval = 0
        self.signal = False


class Prog:
    ENGS = ("pe", "act", "dve", "pool", "sp")

    def __init__(self):
        self.ops = []
        self.eng_ops = {e: [] for e in self.ENGS}
        self.last_w = {}
        self.readers = {}
        self.dma_count = {}
        self._bank = 0

    def next_bank(self):
        b = self._bank
        self._bank = (self._bank + 1) % 8
        return b

    def add(self, eng, fn, reads=(), writes=(), dma_key=None):
        idx = len(self.ops)
        deps = set()
        for r in reads:
            w = self.last_w.get(r)
            if w is not None:
                deps.add(w)
        for wr in writes:
            w = self.last_w.get(wr)
            if w is not None:
                deps.add(w)
            deps.update(self.readers.get(wr, ()))
        for r in reads:
            self.readers.setdefault(r, []).append(idx)
        for wr in writes:
            self.last_w[wr] = idx
            self.readers[wr] = []
        deps.discard(idx)
        op = _Op(idx, eng, fn, deps, dma_key)
        op.pos = len(self.eng_ops[eng])
        self.eng_ops[eng].append(op)
        if dma_key is not None:
            self.dma_count[dma_key] = self.dma_count.get(dma_key, 0) + 1
            op.sigval = 16 * self.dma_count[dma_key]
        self.ops.append(op)
        return idx

    def emit(self, nc, es):
        ops = self.ops
        for op in ops:
            for d in op.deps:
                dop = ops[d]
                if dop.dma_key is not None:
                    continue
                if op.dma_key is not None or dop.eng != op.eng:
                    dop.signal = True
                elif op.eng != "pe" and op.pos - dop.pos <= 2:
                    dop.signal = True
        eng_sem = {}
        for e in self.ENGS:
            eng_sem[e] = es.enter_context(nc.semaphore("sem_" + e))
            cnt = 0
            for op in self.eng_ops[e]:
                if op.dma_key is None and op.signal:
                    cnt += 1
                    op.sigval = cnt
        dma_sem = {}
        for i, key in enumerate(self.dma_count):
            dma_sem[key] = es.enter_context(nc.semaphore("dsem%d" % i))

        block = es.enter_context(nc.Block())
        handles = {"pe": block.tensor, "act": block.scalar, "dve": block.vector,
                   "pool": block.gpsimd, "sp": block.sync}

        def make(e):
            def body(eng):
                waited = {}
                for op in self.eng_ops[e]:
                    for d in sorted(op.deps):
                        dop = ops[d]
                        if dop.dma_key is not None:
                            sem, val, key = dma_sem[dop.dma_key], dop.sigval, ("d", dop.dma_key)
                        else:
                            if dop.eng == op.eng and op.dma_key is None:
                                if op.eng == "pe" or op.pos - dop.pos > 2:
                                    continue
                            sem, val, key = eng_sem[dop.eng], dop.sigval, ("e", dop.eng)
                        if waited.get(key, 0) >= val:
                            continue
                        waited[key] = val
                        eng.wait_ge(sem, val)
                    if op.fn is None:
                        continue
                    ins = op.fn(eng)
                    if op.dma_key is not None:
                        ins.then_inc(dma_sem[op.dma_key], 16)
                    elif op.signal:
                        ins.then_inc(eng_sem[e], 1)
            return body

        for e in self.ENGS:
            if self.eng_ops[e]:
                handles[e](make(e))


def _head(g, hf, cc):
    return 4 * g + 2 * hf + cc


def build(layers, nseq, tps):
    NT = nseq * tps
    nc = bass.Bass("TRN2", target_bir_lowering=False)

    def dram(name, shape, dt, kind):
        return nc.dram_tensor(name, list(shape), dt, kind=kind).ap()

    x_d = dram("x", [NT * TILE, D_MODEL], F32, "ExternalInput")
    out_d = dram("out", [NT * TILE, D_MODEL], F32, "ExternalOutput")
    gb_d = dram("gb", [128, 4, D_MODEL], F32, "ExternalInput")
    ident_d = dram("ident", [128, 128], F32, "ExternalInput")
    wd = {}
    if 0 in layers:
        wd["w_in"] = dram("w_in", [1024, 3072], F32, "ExternalInput")
        wd["w_co"] = dram("w_co", [1024, 1024], F32, "ExternalInput")
        wd["gu0"] = dram("gu0", [1024, 2 * D_FF], F32, "ExternalInput")
        wd["dn0"] = dram("dn0", [D_FF, 1024], F32, "ExternalInput")
        cw_d = dram("cw", [128, 8, 3], F32, "ExternalInput")
    if 1 in layers:
        wd["w_qkv"] = dram("w_qkv", [1024, 1536], F32, "ExternalInput")
        wd["w_o"] = dram("w_o", [1024, 1024], F32, "ExternalInput")
        wd["gu1"] = dram("gu1", [1024, 2 * D_FF], F32, "ExternalInput")
        wd["dn1"] = dram("dn1", [D_FF, 1024], F32, "ExternalInput")
        qkg_d = dram("qkg", [128, 2], F32, "ExternalInput")
        sk_d = dram("sk", [128, 2048], F32, "ExternalInput")
        bias_d = dram("abias", [128, 2 * 16 * 128], F32, "ExternalInput")
        bones_d = dram("bones", [128, 128], F32, "ExternalInput")

    slabs = []

    def kview(ap2d):
        return ap2d.rearrange("(k p) n -> p k n", p=128)

    def add_slab(layer, kind, idx, pieces):
        slabs.append(dict(layer=layer, kind=kind, idx=idx, pieces=pieces))

    def ffn_slabs(layer, gu, dn):
        for s in range(11):
            pieces = []
            for pair in range(2):
                j = 2 * s + pair
                for t, off in enumerate((j * 128, D_FF + j * 128)):
                    sub = pair * 2 + t
                    pieces.append((sub * 1024, 8, 128, kview(gu[:, off:off + 128])))
            add_slab(layer, "gu", s, pieces)
        for s in range(6):
            nk = 4 if s < 5 else 2
            pieces = [(0, nk, 1024, kview(dn[s * 512:s * 512 + nk * 128, :]))]
            add_slab(layer, "dn", s, pieces)

    if 0 in layers:
        w_in = wd["w_in"]
        for j in range(8):
            pieces = []
            for s, off in enumerate((j * 128, 2048 + j * 128, 1024 + j * 128)):
                pieces.append((s * 1024, 8, 128, kview(w_in[:, off:off + 128])))
            add_slab(0, "cin", j, pieces)
        for h in range(2):
            add_slab(0, "cout", h, [(0, 4, 1024, kview(wd["w_co"][h * 512:(h + 1) * 512, :]))])
        ffn_slabs(0, wd["gu0"], wd["dn0"])
    if 1 in layers:
        w_qkv = wd["w_qkv"]
        for s in range(3):
            pieces = []
            for sub in range(4):
                mc = 4 * s + sub
                for hf in range(2):
                    if mc < 8:
                        g, cc = mc // 2, mc % 2
                        off = _head(g, hf, cc) * 64
                    else:
                        off = 1024 + (mc - 8) * 64
                    pieces.append(("sub", sub, hf, kview(w_qkv[:, off:off + 64])))
            add_slab(1, "qk", s, pieces)
        add_slab(1, "v", 0, [(0, 8, 256, kview(w_qkv[:, 1280:1536]))])
        for h in range(2):
            pieces = []
            for cl in range(4):
                c = 4 * h + cl
                g, cc = c // 2, c % 2
                for hf in range(2):
                    hd = _head(g, hf, cc)
                    pieces.append(("rows", hf * 64, cl * 1024, wd["w_o"][hd * 64:(hd + 1) * 64, :]))
            add_slab(1, "wo", h, pieces)
        ffn_slabs(1, wd["gu1"], wd["dn1"])
    NSLAB = len(slabs)
    wb_d = dram("wb", [NSLAB, 128, SLAB_ELEMS], BF16, "Internal")

    pass_order = list(range(NSLAB))
    stream = pass_order * NT

    with ExitStack() as es:
        def sb(name, shape, dt):
            return es.enter_context(nc.sbuf_tensor(name, list(shape), dt))

        P = Prog()

        xbuf = [sb("xbuf%d" % i, [128, 4, D_MODEL], F32) for i in range(2)]
        hT = sb("hT", [128, 8, TILE], BF16)
        htm = [sb("htm%d" % i, [128, D_MODEL], BF16) for i in range(2)]
        junk = sb("junk", [128, D_MODEL], BF16)
        gb = sb("gb_s", [128, 4, D_MODEL], F32)
        identf = sb("identf", [128, 128], F32)
        identb = sb("identb", [128, 128], BF16)
        epsc = sb("epsc", [128, 1], F32)
        st_ss = sb("st_ss", [128, 8], F32)
        st_sd = sb("st_sd", [128, 8], F32)
        st_rs = sb("st_rs", [128, 8], F32)
        zoT = sb("zoT", [128, 8, TILE], BF16)
        aT = sb("aT", [128, NKF, TILE], BF16)
        scr = [sb("scr%d" % i, [128, TILE], F32) for i in range(8)]
        ring = sb("ring", [128, NB_RING, SLAB_ELEMS], BF16)
        if 0 in layers:
            ubuf = [sb("ubuf%d" % i, [128, TILE + 2], F32) for i in range(2)]
            ucarry = sb("ucarry", [128, 8, 2], F32)
            cw = sb("cw_s", [128, 8, 3], F32)
        if 1 in layers:
            qT = sb("qT", [128, 8, TILE], BF16)
            kT = sb("kT", [128, 4, 128 + TILE], BF16)
            vaug = sb("vaug", [128, 5, 4, 2, 128], BF16)
            sqb = [sb("sqb%d" % i, [128, TILE], BF16) for i in range(2)]
            pexp = [sb("pexp%d" % i, [128, TILE], BF16) for i in range(2)]
            pT = [[sb("pT%d_%d" % (r, kt), [128, TILE], BF16) for kt in range(2)] for r in range(2)]
            Etab = sb("Etab", [128, 2, 4, TILE], BF16)
            sinkexp = sb("sinkexp", [128, 2048], F32)
            qkg = sb("qkg_s", [128, 2], F32)
            bonesf = sb("bonesf", [128, 128], F32)
            bonesb = sb("bonesb", [128, 128], BF16)

        ps = [es.enter_context(nc.psum_tensor("ps%d" % i, [128, TILE], F32)) for i in range(8)]
        print("SBUF bytes remaining per partition:", nc.sbuf_bytes_remaining)

        def PS(b):
            return ("ps", b)

        _scr_i = [0]

        def next_scr():
            i = _scr_i[0]
            _scr_i[0] = (i + 1) % 8
            return i

        P.add("sp", lambda e: e.dma_start(out=gb[:], in_=gb_d), writes=["gb"], dma_key="setup_k1")
        P.add("sp", lambda e: e.dma_start(out=identf[:], in_=ident_d), writes=["identf"], dma_key="setup_k2")
        P.add("dve", lambda e: e.memset(epsc[:], EPS), writes=["epsc"])
        P.add("dve", lambda e: e.tensor_copy(out=identb[:], in_=identf[:]), reads=["identf"], writes=["identb"])
        if 0 in layers:
            P.add("sp", lambda e: e.dma_start(out=cw[:], in_=cw_d), writes=["cw"], dma_key="setup_k3")
        if 1 in layers:
            P.add("sp", lambda e: e.dma_start(out=qkg[:], in_=qkg_d), writes=["qkg"], dma_key="setup_k4")
            P.add("sp", lambda e: e.dma_start(out=sinkexp[:], in_=sk_d), writes=["sinkexp"], dma_key="setup_k5")
            P.add("sp", lambda e: e.dma_start(out=bonesf[:], in_=bones_d), writes=["bonesf"], dma_key="setup_k6")
            P.add("dve", lambda e: e.tensor_copy(out=bonesb[:], in_=bonesf[:]), reads=["bonesf"], writes=["bonesb"])
            P.add("act", lambda e: e.activation(out=sinkexp[:], in_=sinkexp[:], func=AF.Exp),
                  reads=["sinkexp"], writes=["sinkexp"])
            P.add("dve", lambda e: e.memset(vaug[:].rearrange("p a b c d -> p (a b c d)"), 1.0), writes=[("va", i) for i in range(5)])
            for kt in range(2):
                stage = xbuf[1][:].rearrange("p b d -> p (b d)")
                P.add("sp", lambda e, kt=kt, stage=stage: e.dma_start(
                    out=stage[:, 0:2048], in_=bias_d[:, kt * 2048:(kt + 1) * 2048]),
                    writes=[("x", 1, b, h) for b in range(4) for h in range(2)], dma_key="setup2_%d" % kt)
                P.add("act", lambda e, kt=kt, stage=stage: e.activation(
                    out=Etab[:, kt, :, :].rearrange("p g n -> p (g n)"), in_=stage[:, 0:2048], func=AF.Exp),
                    reads=[("x", 1, b, h) for b in range(4) for h in range(2)], writes=["Etab"])

        for si, sl in enumerate(slabs):
            for pc in sl["pieces"]:
                if pc[0] == "rows":
                    _, prow, col0, src = pc
                    dst = wb_d[si, prow:prow + 64, col0:col0 + 1024]
                elif pc[0] == "sub":
                    _, sub, hf, src = pc
                    dst = wb_d[si, :, sub * 1024:(sub + 1) * 1024].rearrange(
                        "p (k n) -> p k n", k=8)[:, :, hf * 64:(hf + 1) * 64]
                else:
                    col0, nk, ncol, src = pc
                    dst = wb_d[si, :, col0:col0 + nk * ncol].rearrange("p (k n) -> p k n", k=nk)
                pi = sl["pieces"].index(pc)
                P.add("pool", lambda e, dst=dst, src=src: e.dma_start(out=dst, in_=src),
                      writes=[("wbw", si, pi)], reads=[], dma_key=("cv", si))

        ring_state = dict(next_load=0, next_use=0)

        def ring_load():
            n = ring_state["next_load"]
            if n >= len(stream):
                return
            ring_state["next_load"] = n + 1
            si = stream[n]
            slot = n % NB_RING
            P.add("sp", lambda e, si=si, slot=slot: e.dma_start(out=ring[:, slot, :], in_=wb_d[si]),
                  reads=[("wbw", si, pi) for pi in range(len(slabs[si]["pieces"]))],
                  writes=[("ring", slot)], dma_key=("ring", slot))

        def ring_acquire(expect_kind):
            n = ring_state["next_use"]
            ring_state["next_use"] = n + 1
            si = stream[n]
            assert slabs[si]["kind"] == expect_kind, (slabs[si]["kind"], expect_kind)
            slot = n % NB_RING
            return slot, ring[:, slot, :]

        for _ in range(NB_RING):
            ring_load()

        def x_load(ti):
            par = ti % 2
            P.add("sp", lambda e: e.dma_start(
                out=xbuf[par][:], in_=x_d[ti * TILE:(ti + 1) * TILE, :].rearrange("(b p) d -> p b d", p=128)),
                writes=[("x", par, b, h) for b in range(4) for h in range(2)], dma_key=("xl", par))

        store_ops = []

        def x_store(ti):
            par = ti % 2
            store_ops.append(P.add("sp", lambda e: e.dma_start(
                out=out_d[ti * TILE:(ti + 1) * TILE, :].rearrange("(b p) d -> p b d", p=128), in_=xbuf[par][:]),
                reads=[("x", par, b, h) for b in range(4) for h in range(2)], dma_key=("xs", par)))

        stat_i = [0]

        def phase_norm(par, gidx):
            for b in range(4):
                xb = xbuf[par][:, b, :]
                s = stat_i[0] % 8
                stat_i[0] += 1
                hb = b % 2
                xres = [("x", par, b, 0), ("x", par, b, 1)]
                P.add("act", lambda e, xb=xb, s=s: e.activation(out=junk[:], in_=xb, func=AF.Square,
                                                                 accum_out=st_ss[:, s:s + 1]),
                      reads=xres, writes=[("ss", s)])
                P.add("act", lambda e, s=s: e.activation(out=st_sd[:, s:s + 1], in_=st_ss[:, s:s + 1], func=AF.Sqrt,
                                                         scale=1.0 / D_MODEL, bias=epsc[:]),
                      reads=[("ss", s), "epsc"], writes=[("sd", s)])
                P.add("dve", lambda e, s=s: e.reciprocal(out=st_rs[:, s:s + 1], in_=st_sd[:, s:s + 1]),
                      reads=[("sd", s)], writes=[("rs", s)])
                P.add("dve", lambda e, xb=xb, s=s, hb=hb: e.scalar_tensor_tensor(
                    out=htm[hb][:], in0=xb, scalar=st_rs[:, s:s + 1], in1=gb[:, gidx, :],
                    op0=ALU.mult, op1=ALU.mult),
                    reads=xres + [("rs", s), "gb"], writes=[("htm", hb)])
                bank = P.next_bank()
                psb = ps[bank][:].bitcast(BF16)
                for k in range(8):
                    P.add("pe", lambda e, k=k, hb=hb, psb=psb: e.transpose(
                        psb[:, k * 128:(k + 1) * 128], htm[hb][:, k * 128:(k + 1) * 128], identb[:]),
                        reads=[("htm", hb), "identb"], writes=[PS(bank)])
                P.add("act", lambda e, b=b, psb=psb: e.activation(
                    out=hT[:, :, b * 128:(b + 1) * 128], in_=psb.rearrange("p (k t) -> p k t", k=8), func=AF.Copy),
                    reads=[PS(bank)], writes=[("hT", b)])

        HT_ALL = [("hT", b) for b in range(4)]

        def ws_group(W, blk0, bank, reads_extra=()):
            for k in range(8):
                P.add("pe", lambda e, k=k: e.matmul(ps[bank][:], lhsT=W[:, (blk0 + k) * 128:(blk0 + k + 1) * 128],
                                                   rhs=hT[:, k, :], start=(k == 0), stop=(k == 7)),
                      reads=HT_ALL + list(reads_extra), writes=[PS(bank)])

        def phase_conv(ti, par):
            first = (ti % tps == 0)
            for j in range(8):
                slot, W = ring_acquire("cin")
                banks = [P.next_bank() for _ in range(3)]
                for s in range(3):
                    ws_group(W, s * 8, banks[s], [("ring", slot)])
                ring_load()
                ub = ubuf[j % 2]
                ubr = ("ub", j % 2)
                ib, iy = next_scr(), next_scr()
                bsb, yb = scr[ib], scr[iy]
                P.add("act", lambda e, bsb=bsb, bk=banks[0]: e.activation(out=bsb[:], in_=ps[bk][:], func=AF.Copy),
                      reads=[PS(banks[0])], writes=[("scr", ib)])
                if first:
                    P.add("pool", lambda e, ub=ub: e.memset(ub[:, 0:2], 0.0), writes=[ubr])
                else:
                    P.add("pool", lambda e, ub=ub, j=j: e.tensor_copy(out=ub[:, 0:2], in_=ucarry[:, j, :]),
                          reads=[("uc", j)], writes=[ubr])
                P.add("dve", lambda e, ub=ub, bsb=bsb, bk=banks[1]: e.tensor_tensor(
                    out=ub[:, 2:TILE + 2], in0=bsb[:], in1=ps[bk][:], op=ALU.mult),
                    reads=[("scr", ib), PS(banks[1]), ubr], writes=[ubr])
                P.add("pool", lambda e, ub=ub, j=j: e.tensor_copy(out=ucarry[:, j, :], in_=ub[:, TILE:TILE + 2]),
                      reads=[ubr], writes=[("uc", j)])
                P.add("pool", lambda e, ub=ub, yb=yb, j=j: e.tensor_scalar(
                    out=yb[:], in0=ub[:, 2:TILE + 2], scalar1=cw[:, j, 2:3], scalar2=None, op0=ALU.mult),
                    reads=[ubr, "cw"], writes=[("scr", iy)])
                P.add("dve", lambda e, ub=ub, yb=yb, j=j: e.scalar_tensor_tensor(
                    out=yb[:], in0=ub[:, 1:TILE + 1], scalar=cw[:, j, 1:2], in1=yb[:], op0=ALU.mult, op1=ALU.add),
                    reads=[ubr, "cw", ("scr", iy)], writes=[("scr", iy)])
                P.add("dve", lambda e, ub=ub, yb=yb, j=j: e.scalar_tensor_tensor(
                    out=yb[:], in0=ub[:, 0:TILE], scalar=cw[:, j, 0:1], in1=yb[:], op0=ALU.mult, op1=ALU.add),
                    reads=[ubr, "cw", ("scr", iy)], writes=[("scr", iy)])
                P.add("dve", lambda e, yb=yb, j=j, bk=banks[2]: e.tensor_tensor(
                    out=zoT[:, j, :], in0=yb[:], in1=ps[bk][:], op=ALU.mult),
                    reads=[("scr", iy), PS(banks[2])], writes=[("zo", j)])

        def phase_resid_proj(par, kind):
            s0, W0 = ring_acquire(kind)
            s1, W1 = ring_acquire(kind)
            for b in range(4):
                for half in range(2):
                    bank = P.next_bank()
                    for k in range(8):
                        W, slot = (W0, s0) if k < 4 else (W1, s1)
                        kk = k % 4
                        P.add("pe", lambda e, W=W, kk=kk, k=k, b=b, half=half, bank=bank: e.matmul(
                            ps[bank][:], lhsT=zoT[:, k, b * 128:(b + 1) * 128],
                            rhs=W[:, kk * 1024 + half * 512:kk * 1024 + half * 512 + 512],
                            start=(k == 0), stop=(k == 7)),
                            reads=[("ring", slot), ("zo", k)], writes=[PS(bank)])
                    xs = xbuf[par][:, b, half * 512:(half + 1) * 512]
                    P.add("dve", lambda e, xs=xs, bank=bank: e.tensor_tensor(out=xs, in0=xs, in1=ps[bank][:], op=ALU.add),
                          reads=[PS(bank), ("x", par, b, half)], writes=[("x", par, b, half)])
            ring_load()
            ring_load()

        def phase_ffn(par):
            for s in range(11):
                slot, W = ring_acquire("gu")
                for pair in range(2):
                    j = 2 * s + pair
                    bg, bu = P.next_bank(), P.next_bank()
                    ws_group(W, (pair * 2) * 8, bg, [("ring", slot)])
                    ws_group(W, (pair * 2 + 1) * 8, bu, [("ring", slot)])
                    isg = next_scr()
                    sg = scr[isg]
                    P.add("act", lambda e, sg=sg, bg=bg: e.activation(out=sg[:], in_=ps[bg][:], func=AF.Silu),
                          reads=[PS(bg)], writes=[("scr", isg)])
                    P.add("dve", lambda e, sg=sg, bu=bu, j=j: e.tensor_tensor(
                        out=aT[:, j, :], in0=sg[:], in1=ps[bu][:], op=ALU.mult),
                        reads=[("scr", isg), PS(bu)], writes=[("aT", j)])
                ring_load()
            for s in range(6):
                slot, W = ring_acquire("dn")
                nk = 4 if s < 5 else 2
                if s < 5:
                    order = [(kk, b, half) for kk in range(nk) for b in range(4) for half in range(2)]
                else:
                    order = [(kk, b, half) for b in range(4) for kk in range(nk) for half in range(2)]
                for kk, b, half in order:
                    k = 4 * s + kk
                    bank = b * 2 + half
                    P.add("pe", lambda e, W=W, kk=kk, k=k, b=b, half=half, bank=bank: e.matmul(
                        ps[bank][:], lhsT=aT[:, k, b * 128:(b + 1) * 128],
                        rhs=W[:, kk * 1024 + half * 512:kk * 1024 + half * 512 + 512],
                        start=(k == 0), stop=(k == NKF - 1)),
                        reads=[("ring", slot), ("aT", k)], writes=[PS(bank)])
                ring_load()
            for b in range(4):
                for half in range(2):
                    bank = b * 2 + half
                    xs = xbuf[par][:, b, half * 512:(half + 1) * 512]
                    P.add("dve", lambda e, xs=xs, bank=bank: e.tensor_tensor(out=xs, in0=xs, in1=ps[bank][:], op=ALU.add),
                          reads=[PS(bank), ("x", par, b, half)], writes=[("x", par, b, half)])
            P._bank = 0

        def phase_attn(ti, par):
            first_tile = (ti % tps == 0)
            if not first_tile:
                P.add("pool", lambda e: e.tensor_copy(out=kT[:, :, 0:128], in_=kT[:, :, TILE:TILE + 128]),
                      reads=[("kT", g) for g in range(4)], writes=["kTp"])
                P.add("pool", lambda e: e.tensor_copy(out=vaug[:, 0, :, :, :], in_=vaug[:, 4, :, :, :]),
                      reads=[("va", 4)], writes=[("va", 0)])
            pending = None

            def finish(pd):
                mc, bank, isq = pd
                bank2 = P.next_bank()
                P.add("pe", lambda e: e.matmul(ps[bank2][:], lhsT=bonesb[:], rhs=sqb[isq][:], start=True, stop=True),
                      reads=[("sqb", isq), "bonesb"], writes=[PS(bank2)])
                isd, irq = next_scr(), next_scr()
                P.add("act", lambda e: e.activation(out=scr[isd][:], in_=ps[bank2][:], func=AF.Sqrt,
                                                    scale=1.0 / HEAD_DIM, bias=epsc[:]),
                      reads=[PS(bank2), "epsc"], writes=[("scr", isd)])
                P.add("dve", lambda e: e.reciprocal(out=scr[irq][:], in_=scr[isd][:]),
                      reads=[("scr", isd)], writes=[("scr", irq)])
                if mc < 8:
                    dst, dres, gcol = qT[:, mc, :], ("qT", mc), 0
                else:
                    g = mc - 8
                    dst, dres, gcol = kT[:, g, 128:128 + TILE], ("kT", g), 1
                P.add("dve", lambda e: e.scalar_tensor_tensor(
                    out=dst, in0=ps[bank][:], scalar=qkg[:, gcol:gcol + 1], in1=scr[irq][:],
                    op0=ALU.mult, op1=ALU.mult),
                    reads=[PS(bank), "qkg", ("scr", irq)], writes=[dres])

            for s in range(3):
                slot, W = ring_acquire("qk")
                for sub in range(4):
                    mc = 4 * s + sub
                    bank = P.next_bank()
                    ws_group(W, sub * 8, bank, [("ring", slot)])
                    isq = mc % 2
                    P.add("act", lambda e, bank=bank, isq=isq: e.activation(out=sqb[isq][:], in_=ps[bank][:], func=AF.Square),
                          reads=[PS(bank)], writes=[("sqb", isq)])
                    if pending is not None:
                        finish(pending)
                    pending = (mc, bank, isq)
                ring_load()
            slot, W = ring_acquire("v")
            for b in range(4):
                bank = P.next_bank()
                for k in range(8):
                    P.add("pe", lambda e, k=k, b=b, bank=bank: e.matmul(
                        ps[bank][:, 0:256], lhsT=hT[:, k, b * 128:(b + 1) * 128], rhs=W[:, k * 256:(k + 1) * 256],
                        start=(k == 0), stop=(k == 7)),
                        reads=[("hT", b), ("ring", slot)], writes=[PS(bank)])
                if b == 0 and pending is not None:
                    finish(pending)
                    pending = None
                src = ps[bank][:, 0:256].rearrange("p (g d) -> p g d", g=4)
                P.add("act", lambda e, b=b, src=src: e.activation(out=vaug[:, b + 1, :, 0, 0:64], in_=src, func=AF.Copy),
                      reads=[PS(bank)], writes=[("va", b + 1)])
                P.add("dve", lambda e, b=b, src=src: e.tensor_copy(out=vaug[:, b + 1, :, 1, 64:128], in_=src),
                      reads=[PS(bank)], writes=[("va", b + 1)])
            ring_load()
            it = 0
            for b in range(4):
                kts = [1] if (first_tile and b == 0) else [0, 1]
                for g in range(4):
                    r = it % 2
                    it += 1
                    for kt in kts:
                        b0, b1 = P.next_bank(), P.next_bank()
                        kc0 = (b + kt) * 128
                        kres = [("kT", g)] + (["kTp"] if (b + kt) == 0 else [])
                        for hf, bank in ((0, b0), (1, b1)):
                            P.add("pe", lambda e, hf=hf, bank=bank, g=g, b=b, kc0=kc0: e.matmul(
                                ps[bank][:, 0:256], lhsT=kT[hf * 64:(hf + 1) * 64, g, kc0:kc0 + 128],
                                rhs=qT[hf * 64:(hf + 1) * 64, 2 * g:2 * g + 2, b * 128:(b + 1) * 128],
                                start=True, stop=True),
                                reads=kres + [("qT", 2 * g), ("qT", 2 * g + 1)], writes=[PS(bank)])
                        pe_ = pexp[kt]
                        pres = ("pexp", kt)
                        for hf, bank in ((0, b0), (1, b1)):
                            P.add("act", lambda e, hf=hf, bank=bank, pe_=pe_: e.activation(
                                out=pe_[:, hf * 256:(hf + 1) * 256], in_=ps[bank][:, 0:256], func=AF.Exp, scale=0.125),
                                reads=[PS(bank)], writes=[(pres, hf)])
                        P.add("pool", lambda e, pe_=pe_, kt=kt, g=g, r=r: e.tensor_tensor(
                            out=pT[r][kt][:], in0=pe_[:], in1=Etab[:, kt, g, :], op=ALU.mult),
                            reads=[(pres, 0), (pres, 1), "Etab"], writes=[("pT", r, kt)])
                    bA, bB = P.next_bank(), P.next_bank()
                    for var, bank in ((0, bA), (1, bB)):
                        for kt in kts:
                            P.add("pe", lambda e, var=var, bank=bank, kt=kt, g=g, b=b, r=r,
                                  st_=(kt == kts[0]), sp_=(kt == kts[-1]): e.matmul(
                                ps[bank][:], lhsT=vaug[:, b + kt, g, var, :], rhs=pT[r][kt][:],
                                start=st_, stop=sp_),
                                reads=[("va", b + kt), ("pT", r, kt)], writes=[PS(bank)])
                    for hf in range(2):
                        rows = slice(hf * 64, (hf + 1) * 64)
                        cols = slice(hf * 256, (hf + 1) * 256)
                        data, den = (bA, bB) if hf == 0 else (bB, bA)
                        idt, ird = next_scr(), next_scr()
                        h0 = 4 * g + 2 * hf
                        sk_ap = sinkexp[rows, h0 * 128:(h0 + 2) * 128].rearrange("p (c t) -> p c t", c=2)
                        P.add("dve", lambda e, rows=rows, cols=cols, den=den, idt=idt, sk_ap=sk_ap: e.tensor_tensor(
                            out=scr[idt][rows, cols].rearrange("p (c t) -> p c t", c=2),
                            in0=ps[den][rows, cols].rearrange("p (c t) -> p c t", c=2), in1=sk_ap, op=ALU.add),
                            reads=[PS(den), "sinkexp"], writes=[("scr", idt)])
                        P.add("dve", lambda e, rows=rows, cols=cols, idt=idt, ird=ird: e.reciprocal(
                            out=scr[ird][rows, cols], in_=scr[idt][rows, cols]),
                            reads=[("scr", idt)], writes=[("scr", ird)])
                        P.add("dve", lambda e, rows=rows, cols=cols, data=data, ird=ird, g=g, b=b: e.tensor_tensor(
                            out=zoT[rows, 2 * g:2 * g + 2, b * 128:(b + 1) * 128],
                            in0=ps[data][rows, cols].rearrange("p (c t) -> p c t", c=2),
                            in1=scr[ird][rows, cols].rearrange("p (c t) -> p c t", c=2), op=ALU.mult),
                            reads=[PS(data), ("scr", ird)], writes=[("zo", 2 * g), ("zo", 2 * g + 1)])

        x_load(0)
        for ti in range(NT):
            par = ti % 2
            if ti + 1 < NT:
                x_load(ti + 1)
            for L in layers:
                phase_norm(par, 2 * L)
                if L == 0:
                    phase_conv(ti, par)
                    phase_resid_proj(par, "cout")
                else:
                    phase_attn(ti, par)
                    phase_resid_proj(par, "wo")
                phase_norm(par, 2 * L + 1)
                phase_ffn(par)
            x_store(ti)
        P.add("sp", None, reads=[("x", p_, b, h) for p_ in range(2) for b in range(4) for h in range(2)],
              writes=[("x", p_, b, h) for p_ in range(2) for b in range(4) for h in range(2)])
        P.emit(nc, es)
    return nc


def _alibi_bias_table():
    j = np.arange(128)[:, None]
    i = np.arange(128)[None, :]
    tab = np.full((128, 2, 16, 128), -30000.0, dtype=np.float32)
    for g in range(4):
        for cb in range(4):
            h = 4 * g + cb
            slope = 2.0 ** (-8.0 * (h + 1) / N_Q)
            cur = np.where(j <= i, -slope * (i - j), -30000.0)
            prev = np.where(j > i, -slope * (i + 128 - j), -30000.0)
            tab[:, 1, h, :] = cur
            tab[:, 0, h, :] = prev
    return tab.reshape(128, 2 * 16 * 128).astype(np.float32)


def _common_inputs(inp, layers):
    m = {}
    gb = np.stack([np.asarray(inp["norm_mixer"][0]), np.asarray(inp["norm_ffn"][0]),
                   np.asarray(inp["norm_mixer"][1]), np.asarray(inp["norm_ffn"][1])], 0)
    m["gb"] = np.ascontiguousarray(np.broadcast_to(gb[None], (128, 4, D_MODEL))).astype(np.float32)
    m["ident"] = np.eye(128, dtype=np.float32)
    if 0 in layers:
        m["w_in"] = np.ascontiguousarray(inp["conv_w_in"][0])
        m["w_co"] = np.ascontiguousarray(inp["conv_w_out"][0])
        m["gu0"] = np.ascontiguousarray(inp["ffn_w_gate_up"][0])
        m["dn0"] = np.ascontiguousarray(inp["ffn_w_down"][0])
        m["cw"] = np.ascontiguousarray(np.asarray(inp["conv_w"][0]).reshape(3, 8, 128).transpose(2, 1, 0))
    if 1 in layers:
        m["w_qkv"] = np.ascontiguousarray(inp["attn_w_qkv"][0])
        m["w_o"] = np.ascontiguousarray(inp["attn_w_o"][0])
        m["gu1"] = np.ascontiguousarray(inp["ffn_w_gate_up"][1])
        m["dn1"] = np.ascontiguousarray(inp["ffn_w_down"][1])
        qg = np.asarray(inp["attn_q_gain"][0])
        kg = np.asarray(inp["attn_k_gain"][0])
        m["qkg"] = np.ascontiguousarray(np.stack([np.tile(qg, 2), np.tile(kg, 2)], 1)).astype(np.float32)
        m["sk"] = np.ascontiguousarray(np.broadcast_to(np.asarray(inp["attn_sinks"][0])[None, :, None], (128, 16, 128))).reshape(128, 2048).astype(np.float32)
        m["abias"] = _alibi_bias_table()
        bo = np.zeros((128, 128), np.float32)
        bo[:64, :64] = 1.0
        bo[64:, 64:] = 1.0
        m["bones"] = bo
    return m


def run_layers(inp, x, layers, n_cores, nseq, tps):
    nc = build(layers, nseq, tps)
    common = _common_inputs(inp, layers)
    in_maps = []
    for c in range(n_cores):
        m = dict(common)
        m["x"] = np.ascontiguousarray(x[c * nseq:(c + 1) * nseq].reshape(nseq * tps * TILE, D_MODEL))
        in_maps.append(m)
    res = run_bass_kernel_spmd(nc, in_maps, core_ids=list(range(n_cores)))
    outs = [np.asarray(r["out"]).reshape(nseq, tps * TILE, D_MODEL) for r in res.results]
    return np.concatenate(outs, 0)


FUSED = True


def kernel(**inputs):
    inp = {k: np.asarray(v) for k, v in inputs.items()}
    x = inp["x"].astype(np.float32, copy=False)
    B, S, _ = x.shape
    n_cores = 8
    nseq = B // n_cores
    tps = S // TILE
    if FUSED:
        y = run_layers(inp, x, [0, 1], n_cores, nseq, tps)
    else:
        y = run_layers(inp, x, [0], n_cores, nseq, tps)
        y = run_layers(inp, y, [1], n_cores, nseq, tps)
    return y.astype(np.float32)
```
